# Optimizing a Trainium2 kernel written in Bass

```python
import math
import jax, jax.numpy as jnp
from jax import lax
import numpy as np

D_MODEL = 1024
BATCH = 16
SEQ = 2048
DEPTH = 4

GRID_W = 64
CTX_LEN = 256
HEAD_DIM = 64
QBLK = 128
WINDOW = 128
ROPE_THETA = 10000.0
ROPE_PAIRS = HEAD_DIM // 4
EPS = 1e-6
SUBLN_EPS = 1e-5
ATTN_SCALE = HEAD_DIM ** -0.5

A_HEADS = D_MODEL // 128
A_KV = A_HEADS // 4
A_G = A_HEADS // A_KV
A_W = A_HEADS * HEAD_DIM
A_KVW = A_KV * HEAD_DIM
B_HEADS = D_MODEL // 256
B_W = B_HEADS * 2 * HEAD_DIM
C_HEADS = D_MODEL // 128
C_KV = C_HEADS // 4
C_G = C_HEADS // C_KV
C_W = C_HEADS * HEAD_DIM
C_KVW = C_KV * HEAD_DIM

IN_SIZES = (A_W, A_KVW, A_KVW, A_W,
            B_W, B_W, B_W, B_W,
            C_W, C_KVW, C_KVW, C_W)
IN_WIDTH = sum(IN_SIZES)

kernel_name = "hybrid_parallel_gqa_diff_window_dit"


def rmsnorm(x, g, eps=EPS):
    xf = x.astype(jnp.float32)
    y = xf * lax.rsqrt(jnp.mean(xf * xf, axis=-1, keepdims=True) + eps)
    return (y * g.astype(jnp.float32)).astype(x.dtype)


def axial_rope_tables(rows, dtype):
    row = jnp.repeat(jnp.arange(rows), GRID_W).astype(jnp.float32)
    col = jnp.tile(jnp.arange(GRID_W), rows).astype(jnp.float32)
    freqs = ROPE_THETA ** (-jnp.arange(ROPE_PAIRS, dtype=jnp.float32) / ROPE_PAIRS)
    ang_r = row[:, None] * freqs
    ang_c = col[:, None] * freqs
    ang = jnp.concatenate([ang_r, ang_r, ang_c, ang_c], axis=-1)
    return jnp.cos(ang).astype(dtype), jnp.sin(ang).astype(dtype)


def rope(x, cos, sin):
    xs = x.reshape(x.shape[:-1] + (2, 2, ROPE_PAIRS))
    rot = jnp.concatenate([-xs[..., 1:, :], xs[..., :1, :]], axis=-2).reshape(x.shape)
    return x * cos[:, None, :] + rot * sin[:, None, :]


def project(h, w_in, qn_g, kn_g, cos, sin):
    B, T, _ = h.shape
    idx = [int(v) for v in np.cumsum(IN_SIZES)[:-1]]
    qa, ka, va, ga, qb, kb, vb, gb, qc, kc, vc, gc = jnp.split(h @ w_in, idx, axis=-1)
    qa = rmsnorm(qa.reshape(B, T, A_HEADS, HEAD_DIM), qn_g)
    ka = rmsnorm(ka.reshape(B, T, A_KV, HEAD_DIM), kn_g)
    qb = qb.reshape(B, T, 2 * B_HEADS, HEAD_DIM)
    kb = kb.reshape(B, T, 2 * B_HEADS, HEAD_DIM)
    qc = qc.reshape(B, T, C_HEADS, HEAD_DIM)
    kc = kc.reshape(B, T, C_KV, HEAD_DIM)
    if cos is not None:
        qa, ka, qb, kb, qc, kc = [rope(t, cos, sin) for t in (qa, ka, qb, kb, qc, kc)]
    return (qa.reshape(B, T, A_KV, A_G, HEAD_DIM), ka, va.reshape(B, T, A_KV, HEAD_DIM), ga,
            qb.reshape(B, T, B_HEADS, 2, HEAD_DIM), kb.reshape(B, T, B_HEADS, 2, HEAD_DIM),
            vb.reshape(B, T, B_HEADS, 2 * HEAD_DIM), gb,
            qc.reshape(B, T, C_KV, C_G, HEAD_DIM), kc, vc.reshape(B, T, C_KV, HEAD_DIM), gc)


def gqa_attend(q, k, v, sink=None):
    s = jnp.einsum('bqkgd,btkd->bkgqt', q, k).astype(jnp.float32) * ATTN_SCALE
    if sink is not None:
        kvh, g = q.shape[2], q.shape[3]
        col = jnp.broadcast_to(sink.reshape(kvh, g)[None, :, :, None, None].astype(jnp.float32),
                               s.shape[:-1] + (1,))
        p = jax.nn.softmax(jnp.concatenate([s, col], axis=-1), axis=-1)[..., :-1]
    else:
        p = jax.nn.softmax(s, axis=-1)
    return jnp.einsum('bkgqt,btkd->bqkgd', p.astype(v.dtype), v)


def diff_attend(q, k, v, lam):
    s = jnp.einsum('bqhcd,bthcd->bhcqt', q, k).astype(jnp.float32) * ATTN_SCALE
    p = jax.nn.softmax(s, axis=-1)
    a = p[:, :, 0] - lam * p[:, :, 1]
    return jnp.einsum('bhqt,bthe->bqhe', a.astype(v.dtype), v)


def diff_post(o, g, lam_init):
    B, T = o.shape[:2]
    return (rmsnorm(o, g, SUBLN_EPS) * (1.0 - lam_init)).reshape(B, T, B_W)


def query_blocked(fn, q):
    B, S = q.shape[:2]
    nblk = S // QBLK
    qb = jnp.moveaxis(q.reshape((B, nblk, QBLK) + q.shape[2:]), 1, 0)
    ob = lax.map(fn, qb)
    return jnp.moveaxis(ob, 0, 1).reshape((B, S) + ob.shape[3:])


def window_attend(q, k, v, kc, vc, sink):
    B, S = q.shape[:2]
    nblk = S // QBLK
    ncx = kc.shape[1]

    def band(t):
        tp = jnp.pad(t, ((0, 0), (QBLK, QBLK), (0, 0), (0, 0))).reshape((B, nblk + 2, QBLK) + t.shape[2:])
        return jnp.moveaxis(jnp.concatenate([tp[:, :-2], tp[:, 1:-1], tp[:, 2:]], axis=2), 1, 0)

    qpos = jnp.arange(S).reshape(nblk, QBLK)
    kpos = (jnp.arange(nblk)[:, None] - 1) * QBLK + jnp.arange(3 * QBLK)[None, :]
    mask = ((jnp.abs(qpos[:, :, None] - kpos[:, None, :]) <= WINDOW)
            & (kpos >= 0)[:, None, :] & (kpos < S)[:, None, :])
    qb = jnp.moveaxis(q.reshape((B, nblk, QBLK) + q.shape[2:]), 1, 0)
    sink_l = sink.reshape(C_KV, C_G).astype(jnp.float32)

    def f(args):
        qblk, kblk, vblk, mblk = args
        s_loc = jnp.einsum('bqkgd,btkd->bkgqt', qblk, kblk).astype(jnp.float32) * ATTN_SCALE
        s_loc = jnp.where(mblk[None, None, None], s_loc, -jnp.inf)
        s_ctx = jnp.einsum('bqkgd,btkd->bkgqt', qblk, kc).astype(jnp.float32) * ATTN_SCALE
        col = jnp.broadcast_to(sink_l[None, :, :, None, None], s_ctx.shape[:-1] + (1,))
        p = jax.nn.softmax(jnp.concatenate([s_ctx, s_loc, col], axis=-1), axis=-1)
        p_ctx = p[..., :ncx].astype(v.dtype)
        p_loc = p[..., ncx:-1].astype(v.dtype)
        return (jnp.einsum('bkgqt,btkd->bqkgd', p_ctx, vc)
                + jnp.einsum('bkgqt,btkd->bqkgd', p_loc, vblk))

    ob = lax.map(f, (qb, band(k), band(v), mask))
    return jnp.moveaxis(ob, 0, 1).reshape((B, S) + ob.shape[3:])


def merge_out(h, oa, ob, oc, ga, gb, gc, w_br_a, w_br_b, w_br_c, w_mg, b_mg, w_out):
    B, T, _ = h.shape
    pa = (oa.reshape(B, T, A_W) * jax.nn.silu(ga)) @ w_br_a
    pb = (ob * jax.nn.silu(gb)) @ w_br_b
    pc = (oc.reshape(B, T, C_W) * jax.nn.silu(gc)) @ w_br_c
    g_a, g_b, g_c = jnp.split(jax.nn.sigmoid(h @ w_mg + b_mg), 3, axis=-1)
    return (g_a * pa + g_b * pb + g_c * pc) @ w_out


def setup_inputs(seed: int = 0) -> dict:
    key = jax.random.key(seed)
    ks = jax.random.split(key, 24)
    f32 = jnp.float32
    n = lambda k, shape, s: jax.random.normal(k, shape, f32) * s
    D = D_MODEL
    return {
        "x": n(ks[0], (BATCH, SEQ, D), 1.0),
        "c": n(ks[1], (BATCH, D), 1.0),
        "ctx": n(ks[2], (BATCH, CTX_LEN, D), 1.0),
        "c_ctx": n(ks[3], (D,), 1.0),
        "w_ada": n(ks[4], (DEPTH, D, 3 * D), 0.5 * D ** -0.5),
        "b_ada": n(ks[5], (DEPTH, 3 * D), 0.02),
        "g_pre": 1.0 + n(ks[6], (DEPTH, D), 0.02),
        "g_post": 1.0 + n(ks[7], (DEPTH, D), 0.02),
        "w_in": n(ks[8], (DEPTH, D, IN_WIDTH), D ** -0.5),
        "q_norm": 1.0 + n(ks[9], (DEPTH, HEAD_DIM), 0.02),
        "k_norm": 1.0 + n(ks[10], (DEPTH, HEAD_DIM), 0.02),
        "lam_q1": n(ks[11], (DEPTH, HEAD_DIM), 0.1),
        "lam_k1": n(ks[12], (DEPTH, HEAD_DIM), 0.1),
        "lam_q2": n(ks[13], (DEPTH, HEAD_DIM), 0.1),
        "lam_k2": n(ks[14], (DEPTH, HEAD_DIM), 0.1),
        "subln": 1.0 + n(ks[15], (DEPTH, 2 * HEAD_DIM), 0.02),
        "sink": n(ks[16], (DEPTH, C_HEADS), 0.5),
        "w_br_a": n(ks[17], (DEPTH, A_W, D), A_W ** -0.5),
        "w_br_b": n(ks[18], (DEPTH, B_W, D), B_W ** -0.5),
        "w_br_c": n(ks[19], (DEPTH, C_W, D), C_W ** -0.5),
        "w_mg": n(ks[20], (DEPTH, D, 3 * D), D ** -0.5),
        "b_mg": n(ks[21], (DEPTH, 3 * D), 0.1),
        "w_out": n(ks[22], (DEPTH, D, D), D ** -0.5),
    }


def reference(x, c, ctx, c_ctx, w_ada, b_ada, g_pre, g_post, w_in, q_norm, k_norm,
              lam_q1, lam_k1, lam_q2, lam_k2, subln, sink, w_br_a, w_br_b, w_br_c,
              w_mg, b_mg, w_out):
    S = x.shape[1]
    ROWS = S // GRID_W
    cos, sin = axial_rope_tables(ROWS, x.dtype)
    sc = jax.nn.silu(c)
    scc = jax.nn.silu(c_ctx)
    cx = ctx
    for l in range(DEPTH):
        last = l == DEPTH - 1
        shift, scale, gate = jnp.split(sc @ w_ada[l] + b_ada[l], 3, axis=-1)
        shift_c, scale_c, gate_c = jnp.split(scc @ w_ada[l] + b_ada[l], 3, axis=-1)
        h = rmsnorm(x, g_pre[l]) * (1.0 + scale[:, None]) + shift[:, None]
        hc = rmsnorm(cx, g_pre[l]) * (1.0 + scale_c) + shift_c
        (qa, ka, va, ga, qb, kb, vb, gb, qc, kc, vc, gc) = project(h, w_in[l], q_norm[l], k_norm[l], cos, sin)
        (cqa, cka, cva, cga, cqb, ckb, cvb, cgb, cqc, ckc, cvc, cgc) = project(hc, w_in[l], q_norm[l], k_norm[l], None, None)
        lam_init = 0.8 - 0.6 * math.exp(-0.3 * l)
        lam = (jnp.exp(jnp.sum(lam_q1[l].astype(jnp.float32) * lam_k1[l].astype(jnp.float32)))
               - jnp.exp(jnp.sum(lam_q2[l].astype(jnp.float32) * lam_k2[l].astype(jnp.float32)))
               + lam_init)
        ka_all = jnp.concatenate([cka, ka], axis=1)
        va_all = jnp.concatenate([cva, va], axis=1)
        kb_all = jnp.concatenate([ckb, kb], axis=1)
        vb_all = jnp.concatenate([cvb, vb], axis=1)
        oa = query_blocked(lambda qblk: gqa_attend(qblk, ka_all, va_all), qa)
        ob = diff_post(query_blocked(lambda qblk: diff_attend(qblk, kb_all, vb_all, lam), qb), subln[l], lam_init)
        oc = window_attend(qc, kc, vc, ckc, cvc, sink[l])
        y = merge_out(h, oa, ob, oc, ga, gb, gc, w_br_a[l], w_br_b[l], w_br_c[l], w_mg[l], b_mg[l], w_out[l])
        if not last:
            coa = gqa_attend(cqa, cka, cva)
            cob = diff_post(diff_attend(cqb, ckb, cvb, lam), subln[l], lam_init)
            coc = gqa_attend(cqc, ckc, cvc, sink[l])
            yc = merge_out(hc, coa, cob, coc, cga, cgb, cgc, w_br_a[l], w_br_b[l], w_br_c[l], w_mg[l], b_mg[l], w_out[l])
            cx = cx + gate_c * rmsnorm(yc, g_post[l])
        x = x + gate[:, None] * rmsnorm(y, g_post[l])
    return x
```

```python
import math
from contextlib import ExitStack

import numpy as np
import concourse.bass as bass
import concourse.mybir as mybir
from concourse.bass_utils import run_bass_kernel_spmd

F32 = mybir.dt.float32
BF16 = mybir.dt.bfloat16
AF = mybir.ActivationFunctionType
ALU = mybir.AluOpType
AX = mybir.AxisListType

import os
OPT_ACTH = int(os.environ.get("OPT_ACTH", "1"))
OPT_POOL = int(os.environ.get("OPT_POOL", "0"))
MODE = "fused"
NCORES = 8
D = 1024
SEQ = 2048
CTX = 256
DEPTH = 4
NT = 18
EPS = 1e-6
SUBLN_EPS = 1e-5
SCALE = 0.125
NA = 17
NM = 8
PO = {}


def _pack_layout(nl, nb):
    off = 0
    for name, n in (("b_ada", nl * 24), ("g_pre", nl * 8), ("g_post", nl * 8), ("b_mg", nl * 24),
                    ("subln", nl), ("c", (nb + 1) * 8), ("q_norm", nl * 64), ("k_norm", nl * 64),
                    ("lq1", nl * 64), ("lk1", nl * 64), ("lq2", nl * 64), ("lk2", nl * 64),
                    ("sink", nl * 8)):
        PO[name] = off
        off += n
    return off


class Buf:
    __slots__ = ("w", "r", "excl")

    def __init__(self, excl=False):
        self.w = {}
        self.r = {}
        self.excl = excl


class Sched:
    LIMIT = 28000

    def __init__(self, nc, es):
        self.nc = nc
        self.es = es
        self.eng = {"pe": nc.tensor, "act": nc.scalar, "dve": nc.vector, "pool": nc.gpsimd, "sp": nc.sync}
        self.cur = {}
        self.sems = {}
        self.waited = {}
        self.nsem = 0

    def _prod(self, key):
        st = self.cur.get(key)
        if st is None or st[1] >= self.LIMIT:
            idx = 0 if st is None else st[0] + 1
            sem = self.es.enter_context(self.nc.semaphore("s%d" % self.nsem))
            self.nsem += 1
            self.sems[(key, idx)] = sem
            st = [idx, 0]
            self.cur[key] = st
        return st

    def _wait(self, consumer, key, idx, n):
        k = (consumer, key, idx)
        if self.waited.get(k, 0) >= n:
            return
        self.eng[consumer].wait_ge(self.sems[(key, idx)], n)
        self.waited[k] = n

    def _deps(self, consumer, reads, writes, selfkey=None):
        need = {}

        def add(key, ev):
            o = need.get((key, ev[0]))
            if o is None or o < ev[1]:
                need[(key, ev[0])] = ev[1]
        for b in reads:
            for key, ev in b.w.items():
                if consumer == "pe" and key == "pe":
                    continue
                add(key, ev)
            if b.excl:
                for key, ev in b.r.items():
                    if key != consumer:
                        add(key, ev)
        for b in writes:
            for key, ev in b.w.items():
                if key == consumer or key == selfkey:
                    continue
                add(key, ev)
            for key, ev in b.r.items():
                if key == consumer:
                    continue
                add(key, ev)
        for (key, idx), n in need.items():
            self._wait(consumer, key, idx, n)

    def _record(self, key, ev, reads, writes):
        for b in reads:
            b.r[key] = ev
        for b in writes:
            b.w = {key: ev}
            b.r = {}

    def op(self, engname, fn, reads=(), writes=()):
        reads, writes = _bufs(reads), _bufs(writes)
        self._deps(engname, reads, writes)
        ins = fn()
        st = self._prod(engname)
        st[1] += 1
        ins.then_inc(self.sems[(engname, st[0])], 1)
        self._record(engname, (st[0], st[1]), reads, writes)

    def dma(self, issuer, chan, out, in_, reads=(), writes=()):
        key = "dma:" + chan
        reads, writes = _bufs(reads), _bufs(writes)
        self._deps(issuer, reads, writes, selfkey=key)
        st = self._prod(key)
        st[1] += 16
        self.eng[issuer].dma_start(out=out, in_=in_).then_inc(self.sems[(key, st[0])], 16)
        self._record(key, (st[0], st[1]), reads, writes)

    def wait_all(self, consumer, bufs):
        bufs = _bufs(bufs)
        self._deps(consumer, bufs, bufs)


class Gen:
    __slots__ = ("ring", "k", "gen")

    def __init__(self, ring, k, gen):
        self.ring, self.k, self.gen = ring, k, gen

    @property
    def buf(self):
        assert self.ring.gen[self.k] == self.gen, "ring buffer used after it was re-allocated (lifetime bug)"
        return self.ring.bufs[self.k]


def _bufs(lst):
    return [b.buf if isinstance(b, Gen) else b for b in lst]


class Ring:
    def __init__(self, aps):
        self.aps = aps
        self.bufs = [Buf() for _ in aps]
        self.gen = [0 for _ in aps]
        self.i = 0

    def next(self):
        k = self.i % len(self.aps)
        self.i += 1
        self.gen[k] += 1
        return self.aps[k], Gen(self, k, self.gen[k])


def build(nl, nb, last_flags):
    nc = bass.Bass("TRN2", target_bir_lowering=False)
    NP = _pack_layout(nl, nb)
    x_in = nc.dram_tensor("x", [nb, SEQ, D], F32, kind="ExternalInput").ap()
    c_in = nc.dram_tensor("ctx", [nb, CTX, D], F32, kind="ExternalInput").ap()
    wa_in = nc.dram_tensor("wa", [nl * NA, 128, 4096], F32, kind="ExternalInput").ap()
    wm_in = nc.dram_tensor("wm", [nl * NM, 128, 4608], F32, kind="ExternalInput").ap()
    pp_in = nc.dram_tensor("pp", [128, NP], F32, kind="ExternalInput").ap()
    cc_in = nc.dram_tensor("cc", [128, 16 * 64 * 2 + 3 * 128], F32, kind="ExternalInput").ap()
    out = nc.dram_tensor("out", [nb, SEQ, D], F32, kind="ExternalOutput").ap()
    cxo = nc.dram_tensor("cxo", [nb, CTX, D], F32, kind="ExternalOutput").ap()
    wab = nc.dram_tensor("wab", [nl * NA, 128, 4096], BF16, kind="Internal").ap()
    wmb = nc.dram_tensor("wmb", [nl * NM, 128, 4608], BF16, kind="Internal").ap()
    hts = nc.dram_tensor("hts", [nb, NT, 128, D], BF16, kind="Internal").ap()

    es = ExitStack()
    with es:
        def sb(name, shape, dt):
            return es.enter_context(nc.sbuf_tensor(name, shape, dt))

        KAC = sb("KAC", [128, 2, NT * 128], BF16)
        KB = sb("KB", [128, 4, NT * 128], BF16)
        VA = sb("VA", [128, NT, 2, 128], BF16)
        VC = sb("VC", [128, NT, 2, 128], BF16)
        VB = sb("VB", [128, NT, 4, 128], BF16)
        hT = sb("hT", [128, 8, 512], BF16)
        QAC = sb("QAC", [128, 8, 512], BF16)
        QB = sb("QB", [128, 4, 512], BF16)
        OG = sb("OG", [128, 12, 512], BF16)
        MG = QAC
        xt_t = [sb("xt%d" % i, [128, D], F32) for i in range(3)]
        xo_t = [sb("xo%d" % i, [128, D], F32) for i in range(2)]
        wr_t = [sb("wr%d" % i, [128, 4608], BF16) for i in range(4)]
        f32_t = [sb("f%d" % i, [128, 512], F32) for i in range(7)]
        b16_t = [sb("h%d" % i, [128, 2, 512], BF16) for i in range(3)]
        xn_t = [sb("xn%d" % i, [128, D], BF16) for i in range(2)]
        sac_t = [sb("sac%d" % i, [128, 8, 2, 64], BF16) for i in range(2)]
        sbq_t = [sb("sbq%d" % i, [128, 512], BF16) for i in range(2)]
        sm_t = [sb("sm%d" % i, [128, 16], F32) for i in range(8)]
        GG1 = sb("GG", [128, D], F32)
        GG = [GG1, GG1]
        PP = sb("PP", [128, NP], F32)
        CC = sb("CC", [128, 16 * 64 * 2 + 3 * 128], F32)
        ident_b = sb("identb", [128, 128], BF16)
        ones_b = sb("onesb", [128, 128], BF16)
        ones_f = sb("onesf", [128, 128], F32)
        maskP = sb("maskP", [128, 4, 128], BF16)
        maskN = sb("maskN", [128, 4, 128], BF16)
        scT = sb("scT", [128, 8, nb + 1], BF16)
        mod = sb("mod", [128, 24, 2], F32)
        Ac = sb("Ac", [128, 8, 2], F32)
        gg = sb("gg", [128, 8, 2], F32)
        hb = sb("hb", [128, nl * 24], F32)
        nlam = sb("nlam", [128, nl], F32)
        subc = sb("subc", [128, nl], F32)
        esink = sb("esink", [128, nl * 8], F32)
        ps = es.enter_context(nc.psum_tensor("ps", [128, 8, 512], F32))
        es.enter_context(nc.Block())

        S = Sched(nc, es)
        PE, ACT, DVE, POOL = nc.tensor, nc.scalar, nc.vector, nc.gpsimd
        bank = [Buf(excl=True) for _ in range(8)]
        xt = Ring([t[:] for t in xt_t])
        xo = Ring([t[:] for t in xo_t])
        fp = Ring([t[:] for t in f32_t])
        hp = Ring([t[:] for t in b16_t])
        xn = Ring([t[:] for t in xn_t])
        sac = Ring([t[:] for t in sac_t])
        sbq = Ring([t[:] for t in sbq_t])
        sm = Ring([t[:] for t in sm_t])
        cst = Buf()
        kbuf = [Buf() for _ in range(NT)]
        vbuf = [Buf() for _ in range(NT)]
        hTb2 = [[Buf(), Buf()] for _ in range(4)]
        hTb = hTb2
        qacb = [Buf() for _ in range(4)]
        qbb = [Buf() for _ in range(4)]
        ogb = [Buf() for _ in range(12)]
        mgb = [Buf() for _ in range(8)]
        ggb1 = Buf()
        ggb = [ggb1, ggb1]
        modb = Buf()
        xd = [[Buf() for _ in range(NT)] for _ in range(nb)]
        castb = [[Buf(), Buf()] for _ in range(nl)]
        hd = [[Buf() for _ in range(NT)] for _ in range(nb)]
        xt_chan = {id(b): "x%d" % i for i, b in enumerate(xt.bufs)}
        xo_chan = {id(b): "s%d" % i for i, b in enumerate(xo.bufs)}

        cast_q = {l: [("a", u) for u in range(NA)] + [("m", u) for u in range(NM)] for l in range(nl)}

        def issue_casts(l, n=None):
            q = cast_q[l]
            k = len(q) if n is None else min(n, len(q))
            for _ in range(k):
                kind, u = q.pop(0)
                if kind == "a":
                    g_ = 0 if u < 9 else 1
                    S.dma("pool", "cast%d_%d" % (l, g_), wab[l * NA + u], wa_in[l * NA + u], writes=[castb[l][g_]])
                else:
                    S.dma("pool", "cast%d_1" % l, wmb[l * NM + u], wm_in[l * NM + u], writes=[castb[l][1]])
        issue_casts(0)
        S.dma("sp", "const", PP[:], pp_in, writes=[cst])
        S.dma("sp", "const", CC[:], cc_in, writes=[cst])
        cosT = CC[:, 0:1024].rearrange("p (t d) -> p t d", d=64)
        sinT = CC[:, 1024:2048].rearrange("p (t d) -> p t d", d=64)
        c0 = 2048
        S.op("dve", lambda: DVE.tensor_copy(out=ident_b[:], in_=CC[:, c0:c0 + 128]), [cst], [cst])
        for h in range(4):
            S.op("dve", lambda h=h: DVE.tensor_scalar(out=maskP[:, h, :], in0=CC[:, c0 + 128:c0 + 256], scalar1=1.0, scalar2=30000.0,
                                                      op0=ALU.subtract, op1=ALU.mult), [cst], [cst])
            S.op("dve", lambda h=h: DVE.tensor_scalar(out=maskN[:, h, :], in0=CC[:, c0 + 256:c0 + 384], scalar1=1.0, scalar2=30000.0,
                                                      op0=ALU.subtract, op1=ALU.mult), [cst], [cst])
        S.op("dve", lambda: DVE.memset(ones_b[:], 1.0), [], [cst])
        S.op("dve", lambda: DVE.memset(ones_f[:], 1.0), [], [cst])
        for t in range(NT):
            S.op("pool", lambda t=t: POOL.memset(VA[:, t, :, 64:128], 1.0), [], [vbuf[t]])
            S.op("pool", lambda t=t: POOL.memset(VC[:, t, :, 64:128], 1.0), [], [vbuf[t]])
        ncnd = (nb + 1) * 8
        cs = PP[:, PO["c"]:PO["c"] + ncnd]
        f0, f0b = fp.next()
        S.op("act", lambda: ACT.activation(out=f0[:, 0:ncnd], in_=cs, func=AF.Tanh, scale=0.5), [cst], [f0b])
        S.op("dve", lambda: DVE.scalar_tensor_tensor(out=f0[:, 0:ncnd], in0=f0[:, 0:ncnd], scalar=1.0, in1=cs,
                                                     op0=ALU.add, op1=ALU.mult), [f0b, cst], [f0b])
        S.op("dve", lambda: DVE.tensor_scalar(out=scT[:].rearrange("p k b -> p b k"),
                                              in0=f0[:, 0:ncnd].rearrange("p (b k) -> p b k", k=8),
                                              scalar1=0.5, scalar2=None, op0=ALU.mult), [f0b], [cst])
        S.op("dve", lambda: DVE.tensor_scalar(out=hb[:], in0=PP[:, PO["b_mg"]:PO["b_mg"] + nl * 24], scalar1=0.5,
                                              scalar2=None, op0=ALU.mult), [cst], [cst])
        for l in range(nl):
            lam_init = 0.8 - 0.6 * math.exp(-0.3 * LAYER0[0 + l])
            s1, s1b = sm.next()
            f1, f1b = fp.next()
            for k, (qa_, ka_) in enumerate((("lq1", "lk1"), ("lq2", "lk2"))):
                S.op("dve", lambda k=k, qa_=qa_, ka_=ka_: DVE.tensor_tensor(
                    out=f1[:, k * 64:(k + 1) * 64], in0=PP[:, PO[qa_] + l * 64:PO[qa_] + (l + 1) * 64],
                    in1=PP[:, PO[ka_] + l * 64:PO[ka_] + (l + 1) * 64], op=ALU.mult), [cst], [f1b])
            S.op("dve", lambda: DVE.tensor_reduce(out=s1[:, 0:2], in_=f1[:, 0:128].rearrange("p (k d) -> p k d", d=64),
                                                  axis=AX.X, op=ALU.add), [f1b], [s1b])
            S.op("act", lambda: ACT.activation(out=s1[:, 2:4], in_=s1[:, 0:2], func=AF.Exp), [s1b], [s1b])
            S.op("dve", lambda: DVE.scalar_tensor_tensor(out=nlam[:, l:l + 1], in0=s1[:, 3:4], scalar=-lam_init,
                                                         in1=s1[:, 2:3], op0=ALU.add, op1=ALU.subtract), [s1b], [cst])
            S.op("dve", lambda: DVE.tensor_scalar(out=subc[:, l:l + 1], in0=PP[:, PO["subln"] + l:PO["subln"] + l + 1],
                                                  scalar1=(1.0 - lam_init) * 0.25, scalar2=None, op0=ALU.mult), [cst], [cst])
        S.op("act", lambda: ACT.activation(out=esink[:], in_=PP[:, PO["sink"]:PO["sink"] + nl * 8], func=AF.Exp), [cst], [cst])

        order = []
        for b in range(nb):
            for l in range(nl):
                order += [("a", l, u) for u in range(9)]
                for blk in range(5):
                    if blk == 0 and last_flags[l]:
                        continue
                    order += [("a", l, u) for u in (9, 10, 11, 12, 13, 14)]
                    order += [("m", l, m) for m in range(8)]
                    order += [("a", l, 15), ("a", l, 16)]
        wbufs = [Buf() for _ in range(4)]
        wstate = {"issued": 0, "taken": 0, "released": 0}

        def w_issue():
            n = wstate["issued"]
            kind, l, u = order[n]
            assert not cast_q[l], "weight cast of layer %d not fully emitted before its first use" % l
            slot = n % 4
            if kind == "a":
                S.dma("sp", "w%d" % slot, wr_t[slot][:, 0:4096], wab[l * NA + u], reads=[castb[l][0 if u < 9 else 1]], writes=[wbufs[slot]])
            else:
                S.dma("sp", "w%d" % slot, wr_t[slot][:, :], wmb[l * NM + u], reads=[castb[l][1]], writes=[wbufs[slot]])
            wstate["issued"] += 1

        def w_pump():
            while wstate["issued"] < len(order) and wstate["issued"] - 4 < wstate["released"]:
                w_issue()

        def w_get(kind, l, u, hold=1):
            n = wstate["taken"]
            assert order[n] == (kind, l, u), (order[n], kind, l, u)
            w_pump()
            assert wstate["issued"] > n, "weight ring over-subscribed"
            wstate["taken"] += 1
            return wr_t[n % 4], wbufs[n % 4]

        def w_rel(k=1):
            wstate["released"] += k
            w_pump()

        def rstd_col(ssap, ssb, n, scale, eps):
            r, rb = sm.next()
            S.op("dve", lambda: DVE.tensor_scalar(out=r[:, 0:n], in0=ssap, scalar1=scale, scalar2=eps, op0=ALU.mult, op1=ALU.add), [ssb], [rb])
            S.op("pool", lambda: POOL.tensor_tensor(out=r[:, 0:n], in0=r[:, 0:n], in1=epst[:, 2:3].to_broadcast([128, n]), op=ALU.pow),
                 [rb, cst], [rb])
            return r, rb

        epst = sb("epst", [128, 4], F32)
        S.op("dve", lambda: DVE.memset(epst[:, 2:3], -0.5), [], [cst])
        S.op("dve", lambda: DVE.memset(epst[:, 0:1], EPS), [], [cst])
        S.op("dve", lambda: DVE.memset(epst[:, 1:2], SUBLN_EPS), [], [cst])

        def eps_ap(eps):
            return epst[:, 0:1] if eps == EPS else epst[:, 1:2]

        def emit_h(xap, xb, j, slot, tb):
            ss, ssb = sm.next()
            xnap, xnb = xn.next()
            S.op("act", lambda: ACT.activation(out=xnap, in_=xap, func=AF.Square, accum_out=ss[:, 0:1]), [xb], [ssb, xnb])
            r, rb = rstd_col(ss[:, 0:1], ssb, 1, 1.0 / D, EPS)
            if OPT_ACTH:
                S.op("act", lambda: ACT.activation(out=xnap, in_=xap, func=AF.Identity, scale=r[:, 0:1]), [xb, rb], [xnb])
            else:
                S.op("dve", lambda: DVE.tensor_scalar(out=xnap, in0=xap, scalar1=r[:, 0:1], scalar2=None, op0=ALU.mult), [xb, rb], [xnb])
            pts = [ps[:, 6 + k, :].bitcast(BF16).rearrange("p (c t) -> p c t", t=128) for k in range(2)]

            def tr():
                ins = None
                for c in range(8):
                    ins = PE.transpose(out=pts[c % 2][:, c // 2, :], in_=xnap[:, c * 128:(c + 1) * 128], identity=ident_b[:])
                return ins
            S.op("pe", tr, [xnb, cst], [bank[6], bank[7]])
            for c in range(8):
                dst = hT[:, c, slot * 128:(slot + 1) * 128]
                src_ = pts[c % 2][:, c // 2, :]
                if c % 2 == 0:
                    S.op("act", lambda c=c, dst=dst, src_=src_: ACT.activation(out=dst, in_=src_, func=AF.Identity,
                                                                               scale=Ac[:, c, j:j + 1], bias=mod[:, c, j:j + 1]),
                         [bank[6], modb], [hTb2[slot][0]])
                else:
                    S.op("dve", lambda c=c, dst=dst, src_=src_: DVE.tensor_scalar(out=dst, in0=src_, scalar1=Ac[:, c, j:j + 1],
                                                                                  scalar2=mod[:, c, j:j + 1], op0=ALU.mult, op1=ALU.add),
                         [bank[7], modb], [hTb2[slot][1]])

        def rope(src, srcb, H, dst, dstb, tt, sbuf_src=False):
            if tt is None:
                S.op("act", lambda: ACT.copy(out=dst, in_=src), srcb, [dstb])
                return
            t1, t1b = fp.next()
            t2, t2b = fp.next()
            t1v = t1[:, 0:H * 64].rearrange("p (h d) -> p h d", d=64)
            t2v = t2[:, 0:H * 64].rearrange("p (h a b d) -> p h a b d", a=2, b=2, d=16)
            s5 = src.rearrange("p h (a b d) -> p h a b d", a=2, b=2, d=16)
            cosb = cosT[:, tt, :].unsqueeze(1).to_broadcast([128, H, 64])
            sn5 = sinT[:, tt, :].rearrange("p (a b d) -> p a b d", a=2, b=2, d=16)
            if sbuf_src and (OPT_POOL & 1):
                S.op("pool", lambda: POOL.tensor_tensor(out=t1v, in0=src, in1=cosb, op=ALU.mult), srcb + [cst], [t1b])
            else:
                S.op("dve", lambda: DVE.tensor_tensor(out=t1v, in0=src, in1=cosb, op=ALU.mult), srcb + [cst], [t1b])
            for bb in range(2):
                snb = sn5[:, :, bb, :].unsqueeze(1).to_broadcast([128, H, 2, 16])
                S.op("dve", lambda bb=bb, snb=snb: DVE.tensor_tensor(out=t2v[:, :, :, bb, :], in0=s5[:, :, :, 1 - bb, :], in1=snb,
                                                                     op=ALU.mult), srcb + [cst], [t2b])
            pe_, pn_ = (POOL, "pool") if (OPT_POOL & 2) else (DVE, "dve")
            S.op(pn_, lambda: pe_.tensor_tensor(out=dst, in0=t1v, in1=t2[:, 0:H * 64].rearrange("p (h d) -> p h d", d=64),
                                                op=ALU.add), [t1b, t2b], [dstb])

        def headnorm(src, srcb, H, gname, l):
            sq, sqb = fp.next()
            sqv = sq[:, 0:H * 64].rearrange("p (h d) -> p h d", d=64)
            S.op("act", lambda: ACT.activation(out=sqv, in_=src, func=AF.Square), srcb, [sqb])
            ss, ssb = sm.next()
            S.op("dve", lambda: DVE.tensor_reduce(out=ss[:, 0:H], in_=sqv, axis=AX.X, op=ALU.add), [sqb], [ssb])
            r, rb = rstd_col(ss[:, 0:H], ssb, H, 1.0 / 64, EPS)
            S.op("dve", lambda: DVE.tensor_tensor(out=sqv, in0=src, in1=r[:, 0:H].unsqueeze(2).to_broadcast([128, H, 64]),
                                                  op=ALU.mult), srcb + [rb], [sqb])
            g = PP[:, PO[gname] + l * 64:PO[gname] + (l + 1) * 64].unsqueeze(1).to_broadcast([128, H, 64])
            pe_, pn_ = (POOL, "pool") if (OPT_POOL & 4) else (DVE, "dve")
            S.op(pn_, lambda: pe_.tensor_tensor(out=sqv, in0=sqv, in1=g, op=ALU.mult), [sqb, cst], [sqb])
            return sqv, sqb

        def load_x(b, t, l):
            ap, bf = xt.next()
            if t < 2:
                src = (c_in if l == 0 else cxo)[b, t * 128:(t + 1) * 128, :]
            else:
                src = (x_in if l == 0 else out)[b, (t - 2) * 128:(t - 1) * 128, :]
            if any(k == (b, t) for k, _ in pend_st):
                flush_stores()
            S.dma("sp", xt_chan[id(bf.buf)], ap, src, reads=[xd[b][t]], writes=[bf])
            flush_stores()
            return ap, bf

        pend_st = []

        def flush_stores():
            while pend_st:
                _, f = pend_st.pop(0)
                f()

        tb_ctr = [0]
        hT_pref = set()
        for b in range(nb):
            for l in range(nl):
                last = last_flags[l]
                mp = ps[:, 5, 0:48].rearrange("p (c j) -> p c j", j=2)
                for u in range(6):
                    wt, wb_ = w_get("a", l, u)

                    def mm(u=u, wt=wt):
                        ins = None
                        for ccl in range(4):
                            for kc in range(8):
                                ins = PE.matmul(mp[:, u * 4 + ccl, :], lhsT=wt[:, kc * 512 + ccl * 128:kc * 512 + (ccl + 1) * 128],
                                                rhs=scT[:, kc, b:nb + 1:nb - b], start=(kc == 0), stop=(kc == 7))
                        return ins
                    S.op("pe", mm, [wb_, cst], [bank[5]])
                    w_rel()
                S.op("dve", lambda: DVE.tensor_tensor(out=mod[:], in0=mp, in1=PP[:, PO["b_ada"] + l * 24:PO["b_ada"] + (l + 1) * 24]
                                                      .unsqueeze(2).to_broadcast([128, 24, 2]), op=ALU.add),
                     [bank[5], cst], [modb])
                S.op("dve", lambda: DVE.scalar_tensor_tensor(out=Ac[:], in0=mod[:, 8:16, :], scalar=1.0,
                                                             in1=PP[:, PO["g_pre"] + l * 8:PO["g_pre"] + (l + 1) * 8].unsqueeze(2).to_broadcast([128, 8, 2]),
                                                             op0=ALU.add, op1=ALU.mult), [modb, cst], [modb])
                S.op("dve", lambda: DVE.tensor_tensor(out=gg[:], in0=mod[:, 16:24, :],
                                                      in1=PP[:, PO["g_post"] + l * 8:PO["g_post"] + (l + 1) * 8].unsqueeze(2).to_broadcast([128, 8, 2]),
                                                      op=ALU.mult), [modb, cst], [modb])
                def emit_GG(j):
                    for half in range(2):
                        dg, dgb = fp.next()
                        for c4 in range(4):
                            c = half * 4 + c4
                            S.op("dve", lambda c=c, c4=c4: DVE.tensor_scalar(out=dg[:, c4 * 128:(c4 + 1) * 128], in0=CC[:, c0:c0 + 128],
                                                                            scalar1=gg[:, c, j:j + 1], scalar2=None, op0=ALU.mult),
                                 [modb, cst], [dgb])
                        pb = 6 + half

                        def mmg(pb=pb, dg=dg):
                            ins = None
                            for c4 in range(4):
                                ins = PE.matmul(ps[:, pb, c4 * 128:(c4 + 1) * 128], lhsT=ones_f[:], rhs=dg[:, c4 * 128:(c4 + 1) * 128],
                                                start=True, stop=True)
                            return ins
                        S.op("pe", mmg, [dgb, cst], [bank[pb]])
                        S.op("act", lambda pb=pb, half=half: ACT.copy(out=GG[j][:, half * 512:(half + 1) * 512], in_=ps[:, pb, :]),
                             [bank[pb]], [ggb[j]])

                def tile_pipeline(n, A, Bf, C, Df):
                    A(0)
                    Bf(0)
                    if n > 1:
                        A(1)
                    for i in range(n):
                        C(i)
                        if i + 1 < n:
                            Bf(i + 1)
                        if i + 2 < n:
                            A(i + 2)
                        Df(i)

                wkv = [w_get("a", l, 6 + i, hold=3) for i in range(3)]
                st1 = [dict() for _ in range(NT)]

                def s1A(t):
                    d = st1[t]
                    d["j"] = 1 if t < 2 else 0
                    d["tt"] = None if t < 2 else t - 2
                    xap, xb = load_x(b, t, l)
                    d["tb"] = tb_ctr[0]
                    tb_ctr[0] += 1
                    d["slot"] = d["tb"] % 4
                    emit_h(xap, xb, d["j"], d["slot"], d["tb"])
                    sl = d["slot"]
                    S.dma("sp", "hs%d" % sl, hts[b, t].rearrange("p (c q) -> p c q", q=128), hT[:, :, sl * 128:(sl + 1) * 128],
                          reads=hTb2[sl], writes=[hd[b][t]])

                def s1B(t):
                    d = st1[t]
                    slot = d["slot"]
                    b3 = [(d["tb"] % 2) * 3 + i for i in range(3)]
                    d["b3"] = b3

                    def mmkv():
                        ins = None
                        for kc in range(8):
                            for i in range(3):
                                ins = PE.matmul(ps[:, b3[i], :], lhsT=hT[:, kc, slot * 128:(slot + 1) * 128],
                                                rhs=wkv[i][0][:, kc * 512:(kc + 1) * 512], start=(kc == 0), stop=(kc == 7))
                        return ins
                    S.op("pe", mmkv, hTb2[slot] + [w[1] for w in wkv], [bank[i] for i in b3])

                def s1C(t):
                    d = st1[t]
                    b3, tt = d["b3"], d["tt"]
                    p0 = ps[:, b3[0], :]
                    sacap, sacb = sac.next()
                    sbqap, sbqb = sbq.next()
                    d["sac"], d["sbq"] = (sacap, sacb), (sbqap, sbqb)
                    kn, knb = headnorm(p0[:, 0:128].rearrange("p (h d) -> p h d", d=64), [bank[b3[0]]], 2, "k_norm", l)
                    rope(kn, [knb], 2, sacap[:, 0:2, 0, :], sacb, tt, sbuf_src=True)
                    rope(p0[:, 128:256].rearrange("p (h d) -> p h d", d=64), [bank[b3[0]]], 2, sacap[:, 0:2, 1, :], sacb, tt)
                    rope(ps[:, b3[1], :].rearrange("p (h d) -> p h d", d=64), [bank[b3[1]]], 8,
                         sbqap.rearrange("p (h d) -> p h d", d=64), sbqb, tt)
                    S.op("act", lambda: ACT.copy(out=VA[:, t, :, 0:64], in_=p0[:, 256:384].rearrange("p (h d) -> p h d", d=64)),
                         [bank[b3[0]]], [vbuf[t]])
                    S.op("act", lambda: ACT.copy(out=VC[:, t, :, 0:64], in_=p0[:, 384:512].rearrange("p (h d) -> p h d", d=64)),
                         [bank[b3[0]]], [vbuf[t]])
                    S.op("act", lambda: ACT.copy(out=VB[:, t, :, :], in_=ps[:, b3[2], :].rearrange("p (h d) -> p h d", d=128)),
                         [bank[b3[2]]], [vbuf[t]])

                def s1D(t):
                    d = st1[t]
                    (sacap, sacb), (sbqap, sbqb) = d["sac"], d["sbq"]
                    pb = 6 + (d["tb"] % 2)
                    pt = ps[:, pb, :].bitcast(BF16).rearrange("p (c t) -> p c t", t=128)
                    sacf = sacap.rearrange("p h a d -> p (h a d)")

                    def trk():
                        ins = None
                        for c in range(2):
                            ins = PE.transpose(out=pt[:, c, :], in_=sacf[:, c * 128:(c + 1) * 128], identity=ident_b[:])
                        for c in range(4):
                            ins = PE.transpose(out=pt[:, 2 + c, :], in_=sbqap[:, c * 128:(c + 1) * 128], identity=ident_b[:])
                        return ins
                    S.op("pe", trk, [sacb, sbqb, cst], [bank[pb]])
                    S.op("act", lambda: ACT.copy(out=KAC[:, :, t * 128:(t + 1) * 128], in_=pt[:, 0:2, :]), [bank[pb]], [kbuf[t]])
                    S.op("dve", lambda: DVE.tensor_copy(out=KB[:, :, t * 128:(t + 1) * 128], in_=pt[:, 2:6, :]), [bank[pb]], [kbuf[t]])

                tile_pipeline(NT, s1A, s1B, s1C, s1D)
                w_rel(3)
                for blk in range(5):
                    if blk == 0:
                        if last:
                            continue
                        tiles, j, chunks = [0, 1], 1, [0, 1]
                    else:
                        tiles, j, chunks = [2 + (blk - 1) * 4 + i for i in range(4)], 0, list(range(NT))
                    W = len(tiles) * 128
                    TB = len(tiles)
                    if b == 0 and l + 1 < nl and blk >= 3:
                        issue_casts(l + 1)
                    if blk <= 1:
                        emit_GG(j)
                    wq = [w_get("a", l, 9 + i, hold=3) for i in range(3)]
                    st2 = [dict() for _ in range(TB)]

                    def s2A(i):
                        d = st2[i]
                        t = tiles[i]
                        d["tt"] = None if t < 2 else t - 2
                        d["tb"] = tb_ctr[0]
                        tb_ctr[0] += 1
                        if (b, l, t) not in hT_pref:
                            S.dma("sp", "hl%d" % i, hT[:, :, i * 128:(i + 1) * 128], hts[b, t].rearrange("p (c q) -> p c q", q=128),
                                  reads=[hd[b][t]], writes=hTb2[i])

                    def s2B(i):
                        d = st2[i]
                        b3 = [(d["tb"] % 2) * 3 + k for k in range(3)]
                        d["b3"] = b3

                        def mmq():
                            ins = None
                            for kc in range(8):
                                for k in range(3):
                                    ins = PE.matmul(ps[:, b3[k], :], lhsT=hT[:, kc, i * 128:(i + 1) * 128],
                                                    rhs=wq[k][0][:, kc * 512:(kc + 1) * 512], start=(kc == 0), stop=(kc == 7))
                            return ins
                        S.op("pe", mmq, hTb2[i] + [w[1] for w in wq], [bank[k] for k in b3])

                    def s2C(i):
                        d = st2[i]
                        b3, tt = d["b3"], d["tt"]
                        sacap, sacb = sac.next()
                        sbqap, sbqb = sbq.next()
                        d["sac"], d["sbq"] = (sacap, sacb), (sbqap, sbqb)
                        qn, qnb = headnorm(ps[:, b3[0], :].rearrange("p (h d) -> p h d", d=64), [bank[b3[0]]], 8, "q_norm", l)
                        rope(qn, [qnb], 8, sacap[:, :, 0, :], sacb, tt, sbuf_src=True)
                        rope(ps[:, b3[1], :].rearrange("p (h d) -> p h d", d=64), [bank[b3[1]]], 8, sacap[:, :, 1, :], sacb, tt)
                        rope(ps[:, b3[2], :].rearrange("p (h d) -> p h d", d=64), [bank[b3[2]]], 8,
                             sbqap.rearrange("p (h d) -> p h d", d=64), sbqb, tt)

                    def s2D(i):
                        d = st2[i]
                        (sacap, sacb), (sbqap, sbqb) = d["sac"], d["sbq"]
                        sacf = sacap.rearrange("p h a d -> p (h a d)")
                        pt6 = ps[:, 6, :].bitcast(BF16).rearrange("p (c t) -> p c t", t=128)
                        pt7 = ps[:, 7, :].bitcast(BF16).rearrange("p (c t) -> p c t", t=128)

                        def trq():
                            ins = None
                            for c in range(8):
                                ins = PE.transpose(out=pt6[:, c, :], in_=sacf[:, c * 128:(c + 1) * 128], identity=ident_b[:])
                            return ins
                        S.op("pe", trq, [sacb, cst], [bank[6]])
                        S.op("act", lambda: ACT.copy(out=QAC[:, :, i * 128:(i + 1) * 128], in_=pt6[:, 0:8, :]),
                             [bank[6]], [qacb[i]] + mgb)

                        def trq2():
                            ins = None
                            for c in range(4):
                                ins = PE.transpose(out=pt7[:, c, :], in_=sbqap[:, c * 128:(c + 1) * 128], identity=ident_b[:])
                            return ins
                        S.op("pe", trq2, [sbqb, cst], [bank[7]])
                        S.op("dve", lambda: DVE.tensor_copy(out=QB[:, :, i * 128:(i + 1) * 128], in_=pt7[:, 0:4, :]),
                             [bank[7]], [qbb[i]])

                    tile_pipeline(TB, s2A, s2B, s2C, s2D)
                    w_rel(3)
                    hTall = [bb_ for pr in hTb2[0:TB] for bb_ in pr]
                    sctr = [0]
                    actr = [0]

                    def gate_chunk(wt, wb_, jj):
                        sbk = 2 * (sctr[0] % 2)
                        sctr[0] += 1

                        def mmg():
                            ins = None
                            for kc in range(8):
                                ins = PE.matmul(ps[:, sbk, 0:W], lhsT=wt[:, kc * 512 + jj * 128:kc * 512 + (jj + 1) * 128],
                                                rhs=hT[:, kc, 0:W], start=(kc == 0), stop=(kc == 7))
                            return ins
                        S.op("pe", mmg, hTall + [wb_], [bank[sbk]])
                        th, thb = fp.next()
                        S.op("act", lambda: ACT.activation(out=th[:, 0:W], in_=ps[:, sbk, 0:W], func=AF.Tanh, scale=0.5), [bank[sbk]], [thb])
                        S.op("dve", lambda: DVE.scalar_tensor_tensor(out=th[:, 0:W], in0=th[:, 0:W], scalar=1.0, in1=ps[:, sbk, 0:W],
                                                                     op0=ALU.add, op1=ALU.mult), [thb, bank[sbk]], [thb])
                        return th, thb

                    def attn_stream(steps, hooks=None):
                        n = len(steps)
                        LAG = 2
                        if hooks is None:
                            hooks = {}
                        for k in range(n + LAG):
                            if k < n:
                                steps[k][0]()
                                steps[k][1]()
                            for f in hooks.get(k, ()):
                                f()
                            if k - LAG >= 0:
                                steps[k - LAG][2]()
                                if steps[k - LAG][3] is not None:
                                    steps[k - LAG][3]()
                        for k in sorted(hooks):
                            if k >= n + LAG:
                                for f in hooks[k]:
                                    f()

                    def spair():
                        b0_ = 2 * (sctr[0] % 2)
                        sctr[0] += 1
                        return b0_

                    def trickle():
                        if b == 0 and l + 1 < nl:
                            issue_casts(l + 1, 1)

                    wt, wb_ = w_get("a", l, 12)
                    wtA, wbA = wt, wb_
                    steps = []
                    sgd = {}
                    for h in range(8):
                        kv = h // 4
                        ab = 4 + actr[0] % 4
                        actr[0] += 1
                        ngrp = (len(chunks) + 1) // 2
                        for g in range(ngrp):
                            cg = chunks[2 * g:2 * g + 2]

                            def s_emit(cg=cg, kv=kv, h=h, g=g, st={}):
                                if g == 0:
                                    trickle()
                                sb0 = spair()
                                pT, pTb = hp.next()
                                st["v"] = (sb0, pT, pTb)

                                def f():
                                    ins = None
                                    for k, ch in enumerate(cg):
                                        ins = PE.matmul(ps[:, sb0 + k, 0:W], lhsT=KAC[0:64, kv, ch * 128:(ch + 1) * 128],
                                                        rhs=QAC[0:64, h, 0:W], start=True, stop=True)
                                    return ins
                                S.op("pe", f, [kbuf[ch] for ch in cg] + qacb[0:TB], [bank[sb0 + k] for k in range(len(cg))])
                            st_ = s_emit.__defaults__[-1]

                            def e_emit(st=st_, n=len(cg)):
                                sb0, pT, pTb = st["v"]
                                S.op("act", lambda: ACT.activation(out=pT[:, 0:n, 0:W], in_=ps[:, sb0:sb0 + n, 0:W], func=AF.Exp, scale=SCALE),
                                     [bank[sb0 + k] for k in range(n)], [pTb])

                            def pv_emit(st=st_, g=g, cg=cg, kv=kv, ab=ab, ngrp=ngrp):
                                sb0, pT, pTb = st["v"]

                                def f():
                                    ins = None
                                    for k, ch in enumerate(cg):
                                        ins = PE.matmul(ps[:, ab, 0:W], lhsT=VA[:, ch, kv, :], rhs=pT[:, k, 0:W],
                                                        start=(g == 0 and k == 0), stop=(g == ngrp - 1 and k == len(cg) - 1))
                                    return ins
                                S.op("pe", f, [vbuf[ch] for ch in cg] + [pTb], [bank[ab]])

                            def postA(h=h, ab=ab):
                                if h % 2 == 0:
                                    sgd["sg"] = gate_chunk(wtA, wbA, h // 2)
                                sg, sgb = sgd["sg"]
                                rd, rdb = fp.next()
                                S.op("dve", lambda: DVE.reciprocal(out=rd[0:64, 0:W], in_=ps[64:128, ab, 0:W]), [bank[ab]], [rdb])
                                tm, tmb = fp.next()
                                hpar = h % 2
                                S.op("dve", lambda: DVE.scalar_tensor_tensor(out=tm[hpar * 64:(hpar + 1) * 64, 0:W], in0=ps[0:64, ab, 0:W],
                                                                             scalar=0.25, in1=rd[0:64, 0:W], op0=ALU.mult, op1=ALU.mult),
                                     [bank[ab], rdb], [tmb])
                                S.op("pool", lambda: POOL.tensor_tensor(out=OG[hpar * 64:(hpar + 1) * 64, h // 2, 0:W],
                                                                        in0=tm[hpar * 64:(hpar + 1) * 64, 0:W],
                                                                        in1=sg[hpar * 64:(hpar + 1) * 64, 0:W], op=ALU.mult),
                                     [tmb, sgb], [ogb[h // 2]])
                            steps.append((s_emit, e_emit, pv_emit, postA if g == ngrp - 1 else None))
                    attn_stream(steps)

                    w_rel()
                    wt, wb_ = w_get("a", l, 13)
                    wtB, wbB = wt, wb_
                    steps = []
                    hooksB = {}
                    nch = len(chunks)
                    for hh in range(4):
                        for idx, ch in enumerate(chunks):

                            def s_emit(ch=ch, hh=hh, idx=idx, st={}):
                                if idx == 0:
                                    trickle()
                                sb0 = spair()
                                pT, pTb = hp.next()
                                st["v"] = (sb0, pT, pTb)

                                def f():
                                    ins = None
                                    for c in range(2):
                                        ins = PE.matmul(ps[:, sb0 + c, 0:W], lhsT=KB[c * 64:(c + 1) * 64, hh, ch * 128:(ch + 1) * 128],
                                                        rhs=QB[c * 64:(c + 1) * 64, hh, 0:W], start=True, stop=True)
                                    return ins
                                S.op("pe", f, [kbuf[ch]] + qbb[0:TB], [bank[sb0], bank[sb0 + 1]])
                            st_ = s_emit.__defaults__[-1]

                            def e_emit(st=st_):
                                sb0, pT, pTb = st["v"]
                                S.op("act", lambda: ACT.activation(out=pT[:, :, 0:W], in_=ps[:, sb0:sb0 + 2, 0:W], func=AF.Exp, scale=SCALE),
                                     [bank[sb0], bank[sb0 + 1]], [pTb])

                            def pv_emit(st=st_, idx=idx, ch=ch, hh=hh):
                                sb0, pT, pTb = st["v"]

                                def f():
                                    ins = None
                                    for c in range(2):
                                        ins = PE.matmul(ps[:, 4 + c, 0:W], lhsT=VB[:, ch, hh, :], rhs=pT[:, c, 0:W],
                                                        start=(idx == 0), stop=(idx == nch - 1))
                                    for c in range(2):
                                        ins = PE.matmul(ps[:, 6 + c, 0:W], lhsT=ones_b[:], rhs=pT[:, c, 0:W],
                                                        start=(idx == 0), stop=(idx == nch - 1))
                                    return ins
                                S.op("pe", f, [vbuf[ch], pTb, cst], [bank[4], bank[5], bank[6], bank[7]])

                            def postB(hh=hh):
                                o = []
                                for c in range(2):
                                    ot, otb = fp.next()
                                    dt_, dtb = fp.next()
                                    S.op("act", lambda c=c, ot=ot: ACT.copy(out=ot[:, 0:W], in_=ps[:, 4 + c, 0:W]), [bank[4 + c]], [otb])
                                    S.op("dve", lambda c=c, dt_=dt_: DVE.tensor_copy(out=dt_[:, 0:W], in_=ps[:, 6 + c, 0:W]), [bank[6 + c]], [dtb])
                                    o.append((ot, otb, dt_, dtb))
                                sg, sgb = gate_chunk(wtB, wbB, hh)
                                for c in range(2):
                                    ot, otb, dt_, dtb = o[c]
                                    S.op("dve", lambda dt_=dt_: DVE.reciprocal(out=dt_[:, 0:W], in_=dt_[:, 0:W]), [dtb], [dtb])
                                    S.op("dve", lambda ot=ot, dt_=dt_: DVE.tensor_tensor(out=ot[:, 0:W], in0=ot[:, 0:W], in1=dt_[:, 0:W],
                                                                                         op=ALU.mult), [otb, dtb], [otb])
                                o1, o1b, o2, o2b = o[0][0], o[0][1], o[1][0], o[1][1]
                                S.op("dve", lambda: DVE.scalar_tensor_tensor(out=o1[:, 0:W], in0=o2[:, 0:W], scalar=nlam[:, l:l + 1],
                                                                             in1=o1[:, 0:W], op0=ALU.mult, op1=ALU.add), [o1b, o2b, cst], [o1b])

                                def part2():
                                    S.op("act", lambda: ACT.activation(out=o2[:, 0:W], in_=o1[:, 0:W], func=AF.Square), [o1b], [o2b])
                                    sbk = spair()
                                    S.op("pe", lambda: PE.matmul(ps[:, sbk, 0:W], lhsT=ones_f[:], rhs=o2[:, 0:W], start=True, stop=True),
                                         [o2b, cst], [bank[sbk]])
                                    S.op("act", lambda: ACT.activation(out=o2[:, 0:W], in_=ps[:, sbk, 0:W], func=AF.Sqrt, scale=1.0 / 128,
                                                                       bias=eps_ap(SUBLN_EPS)), [bank[sbk], cst], [o2b])
                                    S.op("dve", lambda: DVE.reciprocal(out=o2[:, 0:W], in_=o2[:, 0:W]), [o2b], [o2b])
                                    S.op("dve", lambda: DVE.scalar_tensor_tensor(out=o1[:, 0:W], in0=o1[:, 0:W], scalar=subc[:, l:l + 1],
                                                                                 in1=o2[:, 0:W], op0=ALU.mult, op1=ALU.mult),
                                         [o1b, o2b, cst], [o1b])
                                    S.op("pool", lambda: POOL.tensor_tensor(out=OG[:, 4 + hh, 0:W], in0=o1[:, 0:W], in1=sg[:, 0:W], op=ALU.mult),
                                         [o1b, sgb], [ogb[4 + hh]])
                                hooksB.setdefault((hh + 1) * nch - 1 + 2 + min(9, nch), []).append(part2)
                            steps.append((s_emit, e_emit, pv_emit, postB if idx == nch - 1 else None))
                    attn_stream(steps, hooksB)

                    w_rel()
                    wt, wb_ = w_get("a", l, 14)
                    wtC, wbC = wt, wb_

                    def gate_tile(i):
                        sbk = spair()

                        def mmg():
                            ins = None
                            for jj in range(4):
                                for kc in range(8):
                                    ins = PE.matmul(ps[:, sbk, jj * 128:(jj + 1) * 128],
                                                    lhsT=wtC[:, kc * 512 + jj * 128:kc * 512 + (jj + 1) * 128],
                                                    rhs=hT[:, kc, i * 128:(i + 1) * 128], start=(kc == 0), stop=(kc == 7))
                            return ins
                        S.op("pe", mmg, hTb2[i] + [wbC], [bank[sbk]])
                        th, thb = fp.next()
                        S.op("act", lambda: ACT.activation(out=th[:, :], in_=ps[:, sbk, :], func=AF.Tanh, scale=0.5), [bank[sbk]], [thb])
                        S.op("dve", lambda: DVE.scalar_tensor_tensor(out=th[:, :], in0=th[:, :], scalar=1.0, in1=ps[:, sbk, :],
                                                                     op0=ALU.add, op1=ALU.mult), [thb, bank[sbk]], [thb])
                        return th, thb
                    steps = []
                    sgtd = {}
                    for i, t in enumerate(tiles):
                        if t < 2:
                            cks = [(0, None), (1, None)]
                        else:
                            gi = t - 2
                            cks = [(0, None), (1, None)]
                            if gi > 0:
                                cks.append((t - 1, maskP))
                            cks.append((t, None))
                            if gi < 15:
                                cks.append((t + 1, maskN))
                        for kv in range(2):
                            ab = 4 + actr[0] % 4
                            actr[0] += 1
                            ngrp = (len(cks) + 1) // 2
                            for g in range(ngrp):
                                cg = cks[2 * g:2 * g + 2]

                                def s_emit(cg=cg, kv=kv, i=i, g=g, st={}):
                                    if g == 0 and kv == 0 and i == 0:
                                        sgtd[0] = gate_tile(0)
                                    sb0 = spair()
                                    pT, pTb = hp.next()
                                    st["v"] = (sb0, pT, pTb)

                                    def f():
                                        ins = None
                                        for k, (ch, msk) in enumerate(cg):
                                            ins = PE.matmul(ps[:, sb0 + k, :].rearrange("p (h q) -> p h q", q=128),
                                                            lhsT=KAC[64:128, kv, ch * 128:(ch + 1) * 128],
                                                            rhs=QAC[64:128, 4 * kv:4 * kv + 4, i * 128:(i + 1) * 128],
                                                            start=True, stop=(msk is None))
                                            if msk is not None:
                                                ins = PE.matmul(ps[:, sb0 + k, :].rearrange("p (h q) -> p h q", q=128),
                                                                lhsT=ident_b[:], rhs=msk[:], start=False, stop=True)
                                        return ins
                                    S.op("pe", f, [kbuf[ch] for ch, _ in cg] + [qacb[i], cst], [bank[sb0 + k] for k in range(len(cg))])
                                st_ = s_emit.__defaults__[-1]

                                def e_emit(st=st_, n=len(cg)):
                                    sb0, pT, pTb = st["v"]
                                    S.op("act", lambda: ACT.activation(out=pT[:, 0:n, :], in_=ps[:, sb0:sb0 + n, :], func=AF.Exp, scale=SCALE),
                                         [bank[sb0 + k] for k in range(n)], [pTb])

                                def pv_emit(st=st_, g=g, cg=cg, kv=kv, ab=ab, ngrp=ngrp):
                                    sb0, pT, pTb = st["v"]

                                    def f():
                                        ins = None
                                        for k, (ch, msk) in enumerate(cg):
                                            ins = PE.matmul(ps[:, ab, :], lhsT=VC[:, ch, kv, :], rhs=pT[:, k, :],
                                                            start=(g == 0 and k == 0), stop=(g == ngrp - 1 and k == len(cg) - 1))
                                        return ins
                                    S.op("pe", f, [vbuf[ch] for ch, _ in cg] + [pTb], [bank[ab]])

                                def postC(i=i, kv=kv, ab=ab):
                                    sgt, sgtb = sgtd[i]
                                    accv = ps[:, ab, :].rearrange("p (h q) -> p h q", q=128)
                                    rd, rdb = fp.next()
                                    rdv = rd[0:64, :].rearrange("p (h q) -> p h q", q=128)
                                    S.op("dve", lambda: DVE.tensor_tensor(
                                        out=rdv, in0=accv[64:128, :, :],
                                        in1=esink[64:128, l * 8 + 4 * kv:l * 8 + 4 * kv + 4].unsqueeze(2).to_broadcast([64, 4, 128]),
                                        op=ALU.add), [bank[ab], cst], [rdb])
                                    S.op("dve", lambda: DVE.reciprocal(out=rd[0:64, :], in_=rd[0:64, :]), [rdb], [rdb])
                                    tm, tmb = fp.next()
                                    tmv = tm[:, :].rearrange("p (h q) -> p h q", q=128)
                                    for par in range(2):
                                        S.op("dve", lambda par=par: DVE.scalar_tensor_tensor(out=tmv[par * 64:(par + 1) * 64, par:4:2, :],
                                                                                             in0=accv[0:64, par:4:2, :], scalar=0.25,
                                                                                             in1=rdv[:, par:4:2, :], op0=ALU.mult, op1=ALU.mult),
                                             [bank[ab], rdb], [tmb])
                                    sgv = sgt[:, :].rearrange("p (c q) -> p c q", q=128)
                                    for par in range(2):
                                        S.op("pool", lambda par=par: POOL.tensor_tensor(
                                            out=OG[par * 64:(par + 1) * 64, 8 + 2 * kv:8 + 2 * kv + 2, i * 128:(i + 1) * 128],
                                            in0=tmv[par * 64:(par + 1) * 64, par:4:2, :],
                                            in1=sgv[par * 64:(par + 1) * 64, 2 * kv:2 * kv + 2, :], op=ALU.mult),
                                             [tmb, sgtb], [ogb[8 + 2 * kv], ogb[8 + 2 * kv + 1]])
                                    if kv == 1 and i + 1 < TB:
                                        sgtd[i + 1] = gate_tile(i + 1)
                                steps.append((s_emit, e_emit, pv_emit, postC if g == ngrp - 1 else None))
                    attn_stream(steps)

                    w_rel()
                    pctr = 0
                    for m in range(8):
                        wt, wb_ = w_get("m", l, m)
                        tbr = []
                        for br in range(3):
                            b0 = (pctr % 4) * 2
                            pctr += 1

                            def mmm(br=br, b0=b0, wt=wt):
                                ins = None
                                for kc in range(4):
                                    ins = PE.matmul(ps[:, b0, 0:W], lhsT=wt[:, (br * 4 + kc) * 128:(br * 4 + kc + 1) * 128],
                                                    rhs=OG[:, br * 4 + kc, 0:W], start=(kc == 0), stop=(kc == 3))
                                for kc in range(8):
                                    ins = PE.matmul(ps[:, b0 + 1, 0:W], lhsT=wt[:, (12 + br * 8 + kc) * 128:(12 + br * 8 + kc + 1) * 128],
                                                    rhs=hT[:, kc, 0:W], start=(kc == 0), stop=(kc == 7))
                                return ins
                            S.op("pe", mmm, ogb[br * 4:br * 4 + 4] + hTall + [wb_], [bank[b0], bank[b0 + 1]])
                            th, thb = fp.next()
                            S.op("act", lambda b0=b0, th=th, br=br: ACT.activation(
                                out=th[:, 0:W], in_=ps[:, b0 + 1, 0:W], func=AF.Tanh, scale=0.5,
                                bias=hb[:, l * 24 + br * 8 + m:l * 24 + br * 8 + m + 1]), [bank[b0 + 1], cst], [thb])
                            S.op("dve", lambda b0=b0, th=th: DVE.scalar_tensor_tensor(out=th[:, 0:W], in0=th[:, 0:W], scalar=1.0,
                                                                                      in1=ps[:, b0, 0:W], op0=ALU.add, op1=ALU.mult),
                                 [thb, bank[b0]], [thb])
                            tbr.append((th, thb))
                        S.op("pool", lambda tbr=tbr: POOL.tensor_tensor(out=tbr[0][0][:, 0:W], in0=tbr[0][0][:, 0:W], in1=tbr[1][0][:, 0:W],
                                                                        op=ALU.add), [tbr[0][1], tbr[1][1]], [tbr[0][1]])
                        S.op("pool", lambda tbr=tbr, m=m: POOL.tensor_tensor(out=MG[:, m, 0:W], in0=tbr[0][0][:, 0:W], in1=tbr[2][0][:, 0:W],
                                                                             op=ALU.add), [tbr[0][1], tbr[2][1]], [mgb[m]] + qacb)
                        w_rel()

                    if blk < 4:
                        ntiles = [2 + blk * 4 + i_ for i_ in range(4)]
                        for i_, t_ in enumerate(ntiles):
                            S.dma("sp", "hl%d" % i_, hT[:, :, i_ * 128:(i_ + 1) * 128], hts[b, t_].rearrange("p (c q) -> p c q", q=128),
                                  reads=[hd[b][t_]], writes=hTb2[i_])
                            hT_pref.add((b, l, t_))
                    wo = [w_get("a", l, 15 + i, hold=2) for i in range(2)]
                    for i, t in enumerate(tiles):
                        b0 = (i % 4) * 2

                        def mmo(i=i, b0=b0):
                            ins = None
                            for half in range(2):
                                for kc in range(8):
                                    ins = PE.matmul(ps[:, b0 + half, :], lhsT=MG[:, kc, i * 128:(i + 1) * 128],
                                                    rhs=wo[half][0][:, kc * 512:(kc + 1) * 512], start=(kc == 0), stop=(kc == 7))
                            return ins
                        S.op("pe", mmo, mgb + [w[1] for w in wo], [bank[b0], bank[b0 + 1]])
                        y = ps[:, b0:b0 + 2, :]
                        ss, ssb = sm.next()
                        jk, jkb = xn.next()
                        S.op("act", lambda y=y, ss=ss, jk=jk: ACT.activation(out=jk.rearrange("p (a n) -> p a n", a=2), in_=y, func=AF.Square,
                                                                             accum_out=ss[:, 0:1]), [bank[b0], bank[b0 + 1]], [ssb, jkb])
                        r, rb = rstd_col(ss[:, 0:1], ssb, 1, 1.0 / D, EPS)
                        xap, xb = load_x(b, t, l)
                        xoap, xob = xo.next()
                        S.op("dve", lambda y=y, r=r, xoap=xoap: DVE.scalar_tensor_tensor(
                            out=xoap.rearrange("p (a n) -> p a n", a=2), in0=y, scalar=r[:, 0:1],
                            in1=GG[j][:].rearrange("p (a n) -> p a n", a=2), op0=ALU.mult, op1=ALU.mult),
                             [bank[b0], bank[b0 + 1], rb, ggb[j]], [xob])
                        S.op("dve", lambda xoap=xoap, xap=xap: DVE.tensor_tensor(out=xoap, in0=xoap, in1=xap, op=ALU.add), [xob, xb], [xob])
                        dst = cxo[b, t * 128:(t + 1) * 128, :] if t < 2 else out[b, (t - 2) * 128:(t - 1) * 128, :]
                        pend_st.append(((b, t), lambda xob=xob, dst=dst, xoap=xoap, t=t, b=b: S.dma(
                            "sp", xo_chan[id(xob.buf)], dst, xoap, reads=[xob], writes=[xd[b][t]])))
                    w_rel(2)

        flush_stores()
        allx = [xd[b][t] for b in range(nb) for t in range(NT)]
        S.wait_all("sp", allx)
        S.wait_all("sp", xo.bufs)
    return nc


LAYER0 = [0, 1, 2, 3]


def _unit(w):
    return np.ascontiguousarray(w.reshape(8, 128, 512).transpose(1, 0, 2).reshape(128, 4096))


def _prep_weights(l, w_ada, w_in, w_br_a, w_br_b, w_br_c, w_mg, w_out):
    wi = w_in[l]
    o = dict(qa=0, ka=512, va=640, ga=768, qb=1280, kb=1792, vb=2304, gb=2816, qc=3328, kc=3840, vc=3968, gc=4096)
    WA = np.empty((NA, 128, 4096), np.float32)
    for u in range(6):
        WA[u] = _unit(w_ada[l][:, u * 512:(u + 1) * 512])
    kv0 = np.concatenate([wi[:, o["ka"]:o["ka"] + 128], wi[:, o["kc"]:o["kc"] + 128],
                          wi[:, o["va"]:o["va"] + 128], wi[:, o["vc"]:o["vc"] + 128]], axis=1)
    WA[6] = _unit(kv0)
    WA[7] = _unit(wi[:, o["kb"]:o["kb"] + 512])
    WA[8] = _unit(wi[:, o["vb"]:o["vb"] + 512])
    WA[9] = _unit(wi[:, o["qa"]:o["qa"] + 512])
    WA[10] = _unit(wi[:, o["qc"]:o["qc"] + 512])
    WA[11] = _unit(wi[:, o["qb"]:o["qb"] + 512])
    WA[12] = _unit(wi[:, o["ga"]:o["ga"] + 512])
    WA[13] = _unit(wi[:, o["gb"]:o["gb"] + 512])
    WA[14] = _unit(wi[:, o["gc"]:o["gc"] + 512])
    WA[15] = _unit(w_out[l][:, 0:512])
    WA[16] = _unit(w_out[l][:, 512:1024])
    WM = np.empty((NM, 128, 36, 128), np.float32)
    brs = (w_br_a[l], w_br_b[l], w_br_c[l])
    for m in range(8):
        for br in range(3):
            blk = brs[br][:, m * 128:(m + 1) * 128].reshape(4, 128, 128)
            WM[m, :, br * 4:(br + 1) * 4, :] = blk.transpose(1, 0, 2)
            g = w_mg[l][:, br * 1024 + m * 128:br * 1024 + (m + 1) * 128].reshape(8, 128, 128)
            WM[m, :, 12 + br * 8:12 + (br + 1) * 8, :] = g.transpose(1, 0, 2)
    return WA, WM.reshape(NM, 128, 4608)


def _fm(v, n):
    return np.ascontiguousarray(np.asarray(v, np.float32).reshape(n, 128).T)


def _consts():
    t = np.arange(SEQ)
    row = (t // 64).astype(np.float32)
    col = (t % 64).astype(np.float32)
    freqs = (10000.0 ** (-np.arange(16, dtype=np.float32) / 16)).astype(np.float32)
    ar = row[:, None] * freqs
    ac = col[:, None] * freqs
    ang = np.concatenate([ar, ar, ac, ac], axis=-1).astype(np.float32)
    cos = np.cos(ang).astype(np.float32)
    sin = np.sin(ang).astype(np.float32)
    sgn = np.concatenate([-np.ones(16), np.ones(16), -np.ones(16), np.ones(16)]).astype(np.float32)
    sins = sin * sgn
    CCa = np.zeros((128, 16 * 64 * 2 + 3 * 128), np.float32)
    CCa[:, 0:1024] = cos.reshape(16, 128, 64).transpose(1, 0, 2).reshape(128, 1024)
    CCa[:, 1024:2048] = sins.reshape(16, 128, 64).transpose(1, 0, 2).reshape(128, 1024)
    CCa[:, 2048:2176] = np.eye(128, dtype=np.float32)
    jj = np.arange(128)[:, None]
    pp = np.arange(128)[None, :]
    CCa[:, 2176:2304] = (jj >= pp).astype(np.float32)
    CCa[:, 2304:2432] = (jj <= pp).astype(np.float32)
    return CCa


_CACHE = {}


def _run(layers, xs, cs, inp, nb):
    nl = len(layers)
    key = (tuple(layers), nb)
    LAYER0[:] = list(layers) + [0] * (4 - nl)
    if key not in _CACHE:
        _CACHE[key] = build(nl, nb, [l == DEPTH - 1 for l in layers])
    nc = _CACHE[key]
    NP = _pack_layout(nl, nb)
    WAs, WMs = [], []
    for l in layers:
        WA, WM = _prep_weights(l, inp["w_ada"], inp["w_in"], inp["w_br_a"], inp["w_br_b"], inp["w_br_c"], inp["w_mg"], inp["w_out"])
        WAs.append(WA)
        WMs.append(WM)
    WAs = np.concatenate(WAs, 0)
    WMs = np.concatenate(WMs, 0)
    CCa = _consts()
    in_maps = []
    ncore = len(xs)
    for core in range(ncore):
        P = np.zeros((128, NP), np.float32)
        for k, l in enumerate(layers):
            P[:, PO["b_ada"] + k * 24:PO["b_ada"] + (k + 1) * 24] = _fm(inp["b_ada"][l], 24)
            P[:, PO["g_pre"] + k * 8:PO["g_pre"] + (k + 1) * 8] = _fm(inp["g_pre"][l], 8)
            P[:, PO["g_post"] + k * 8:PO["g_post"] + (k + 1) * 8] = _fm(inp["g_post"][l], 8)
            P[:, PO["b_mg"] + k * 24:PO["b_mg"] + (k + 1) * 24] = _fm(inp["b_mg"][l], 24)
            P[:, PO["subln"] + k] = inp["subln"][l]
            for nm, src in (("q_norm", "q_norm"), ("k_norm", "k_norm"), ("lq1", "lam_q1"), ("lk1", "lam_k1"),
                            ("lq2", "lam_q2"), ("lk2", "lam_k2")):
                P[:, PO[nm] + k * 64:PO[nm] + (k + 1) * 64] = np.broadcast_to(inp[src][l][None, :], (128, 64))
            P[:, PO["sink"] + k * 8:PO["sink"] + (k + 1) * 8] = np.broadcast_to(inp["sink"][l][None, :], (128, 8))
        for bb in range(nb):
            P[:, PO["c"] + bb * 8:PO["c"] + (bb + 1) * 8] = _fm(inp["c"][core * nb + bb], 8)
        P[:, PO["c"] + nb * 8:PO["c"] + (nb + 1) * 8] = _fm(inp["c_ctx"], 8)
        in_maps.append({"x": xs[core], "ctx": cs[core], "wa": WAs, "wm": WMs, "pp": P, "cc": CCa})
    res = run_bass_kernel_spmd(nc, in_maps, core_ids=list(range(ncore)))
    return [np.asarray(r["out"]) for r in res.results], [np.asarray(r["cxo"]) for r in res.results]


def kernel(**inp):
    inp = {k: np.asarray(v) for k, v in inp.items()}
    nb = inp["x"].shape[0] // NCORES
    xs = [np.ascontiguousarray(inp["x"][i * nb:(i + 1) * nb]) for i in range(NCORES)]
    cs = [np.ascontiguousarray(inp["ctx"][i * nb:(i + 1) * nb]) for i in range(NCORES)]
    if MODE == "fused":
        xs, cs = _run([0, 1, 2, 3], xs, cs, inp, nb)
    else:
        for l in range(DEPTH):
            xs, cs = _run([l], xs, cs, inp, nb)
    return np.concatenate(xs, axis=0).astype(np.float32)
```
